# Optimizing a Trainium2 kernel written in Bass

```python
import jax
import jax.numpy as jnp
from jax import lax
import numpy as np

D_MODEL = 2048
BATCH = 4
SEQ = 2048
DEPTH = 4

GRID_W = 64
CTX_LEN = 256
NORM_EPS = 1e-6
N_MOD = 6

GLA_HEADS = 4
GLA_DK = D_MODEL // 2
GLA_DV = D_MODEL
GLA_HDK = GLA_DK // GLA_HEADS
GLA_HDV = GLA_DV // GLA_HEADS
GLA_RANK = 16
GLA_GATE_NORM = 16.0
GLA_CHUNK = 64

ATT_HEADS = 16
ATT_KV_HEADS = 4
ATT_HD = 128
ATT_GROUP = ATT_HEADS // ATT_KV_HEADS
ATT_Q = ATT_HEADS * ATT_HD
ATT_KV = ATT_KV_HEADS * ATT_HD
Q_BLOCK = 128
ROPE_THETA = 10000.0
ROPE_AXIS_PAIRS = ATT_HD // 4

LRU_W = D_MODEL
LRU_BLOCKS = 16
LRU_BS = LRU_W // LRU_BLOCKS
LRU_C = 8.0
CONV_W = 4
CONV_LEFT = 2

N_BRANCH = 3
FFN_HIDDEN = -(-8 * D_MODEL // (3 * 256)) * 256

SPLITS = (GLA_DK, GLA_DK, GLA_DV, GLA_DV, 2 * GLA_RANK, ATT_Q, ATT_KV, ATT_KV, LRU_W, LRU_W, N_BRANCH * D_MODEL)
N_IN = sum(SPLITS)

kernel_name = 'hybrid_gla_gqa_rglru_prefix_dit'


def rmsnorm(x, g):
    xf = x.astype(jnp.float32)
    y = xf * lax.rsqrt(jnp.mean(xf * xf, axis=-1, keepdims=True) + NORM_EPS)
    return (y * g.astype(jnp.float32)).astype(x.dtype)


def modulate(x, shift, scale):
    return x * (1 + scale) + shift


def split_cols(p):
    return jnp.split(p, np.cumsum(SPLITS)[:-1].tolist(), axis=-1)


def flip(z):
    return jnp.flip(z, axis=1)


def axial_rope(n_tokens):
    rows = n_tokens // GRID_W
    row = jnp.repeat(jnp.arange(rows, dtype=jnp.float32), GRID_W)
    col = jnp.tile(jnp.arange(GRID_W, dtype=jnp.float32), rows)
    inv = ROPE_THETA ** (-jnp.arange(ROPE_AXIS_PAIRS, dtype=jnp.float32) / ROPE_AXIS_PAIRS)
    ang = jnp.concatenate([row[:, None] * inv, col[:, None] * inv], axis=-1)
    return jnp.cos(ang), jnp.sin(ang)


def apply_rope(x, cos, sin):
    c_ = cos[None, :, None, :].astype(x.dtype)
    s_ = sin[None, :, None, :].astype(x.dtype)
    x1, x2 = x[..., 0::2], x[..., 1::2]
    return jnp.stack([x1 * c_ - x2 * s_, x1 * s_ + x2 * c_], axis=-1).reshape(x.shape)


def gla_scan(q, k, v, log_a, s0):
    B, T, H, _ = q.shape
    n = T // GLA_CHUNK

    def chunks(z):
        return z.astype(jnp.float32).reshape(B, n, GLA_CHUNK, H, z.shape[-1]).transpose(1, 0, 3, 2, 4)

    qc, kc, vc, gc = chunks(q), chunks(k), chunks(v), chunks(log_a)
    b = jnp.cumsum(gc, axis=3)
    b_last = b[:, :, :, -1:, :]
    q_dec = qc * jnp.exp(b)
    k_dec = kc * jnp.exp(b_last - b)
    lower = jnp.tril(jnp.ones((GLA_CHUNK, GLA_CHUNK), dtype=bool))
    att = jnp.einsum('nbhid,nbhjd->nbhij', q_dec, kc * jnp.exp(-b))
    att = jnp.where(lower, att, 0.0)
    o_intra = jnp.einsum('nbhij,nbhjv->nbhiv', att, vc)

    def step(s, xs):
        qd, kd, vv, dl = xs
        o = jnp.einsum('bhid,bhdv->bhiv', qd, s)
        s = s * dl[:, :, 0, :, None] + jnp.einsum('bhjd,bhjv->bhdv', kd, vv)
        return s, o

    s_fin, o_inter = lax.scan(step, s0, (q_dec, k_dec, vc, jnp.exp(b_last)))
    o = (o_intra + o_inter).transpose(1, 0, 3, 2, 4).reshape(B, T, H, v.shape[-1])
    return o.astype(v.dtype), s_fin


def gla_prep(q, k, v, dec, w_decay, b_decay):
    B, T, _ = q.shape
    la = jax.nn.log_sigmoid(
        jnp.einsum('btnr,nrk->btnk', dec.reshape(B, T, 2, GLA_RANK).astype(jnp.float32), w_decay.astype(jnp.float32))
        + b_decay.astype(jnp.float32)) / GLA_GATE_NORM
    la = la.reshape(B, T, 2, GLA_HEADS, GLA_HDK)
    qh = q.reshape(B, T, GLA_HEADS, GLA_HDK) * (GLA_HDK ** -0.5)
    kh = k.reshape(B, T, GLA_HEADS, GLA_HDK)
    vh = v.reshape(B, T, GLA_HEADS, GLA_HDV)
    return qh, kh, vh, la[:, :, 0], la[:, :, 1]


def gla_out(o, r, norm_g):
    B, T = o.shape[:2]
    return rmsnorm(o, norm_g).reshape(B, T, GLA_DV).astype(r.dtype) * jax.nn.silu(r)


def gla_branch(q, k, v, r, dec, cq, ck, cv, cr, cdec, w_decay, b_decay, norm_g, need_ctx):
    qh, kh, vh, lf, lb = gla_prep(q, k, v, dec, w_decay, b_decay)
    cqh, ckh, cvh, clf, clb = gla_prep(cq, ck, cv, cdec, w_decay, b_decay)
    B = q.shape[0]
    s0 = jnp.zeros((B, GLA_HEADS, GLA_HDK, GLA_HDV), jnp.float32)
    oc_f, s_f = gla_scan(cqh, ckh, cvh, clf, s0)
    oc_b, s_b = gla_scan(flip(cqh), flip(ckh), flip(cvh), flip(clb), s0)
    o_f, _ = gla_scan(qh, kh, vh, lf, s_f)
    o_b, _ = gla_scan(flip(qh), flip(kh), flip(vh), flip(lb), s_b)
    out = gla_out(o_f + flip(o_b), r, norm_g)
    out_c = gla_out(oc_f + flip(oc_b), cr, norm_g) if need_ctx else None
    return out, out_c


def gqa_attend(q, k, v):
    s = jnp.einsum('bqkgd,bskd->bkgqs', q.astype(jnp.float32), k.astype(jnp.float32)) * (ATT_HD ** -0.5)
    p = jax.nn.softmax(s, axis=-1)
    return jnp.einsum('bkgqs,bskd->bqkgd', p.astype(v.dtype), v)


def gqa_branch(q, k, v, cq, ck, cv, q_g, k_g, cos, sin, need_ctx):
    B, T, _ = q.shape
    Lc = cq.shape[1]

    def heads(z, n):
        return z.reshape(z.shape[0], z.shape[1], n, ATT_HD)

    qh = apply_rope(rmsnorm(heads(q, ATT_HEADS), q_g), cos, sin)
    kh = apply_rope(rmsnorm(heads(k, ATT_KV_HEADS), k_g), cos, sin)
    vh = heads(v, ATT_KV_HEADS)
    cqh = rmsnorm(heads(cq, ATT_HEADS), q_g)
    ckh = rmsnorm(heads(ck, ATT_KV_HEADS), k_g)
    cvh = heads(cv, ATT_KV_HEADS)
    k_all = jnp.concatenate([kh, ckh], axis=1)
    v_all = jnp.concatenate([vh, cvh], axis=1)
    nb = T // Q_BLOCK
    qb = qh.reshape(B, nb, Q_BLOCK, ATT_KV_HEADS, ATT_GROUP, ATT_HD).transpose(1, 0, 2, 3, 4, 5)
    o = lax.map(lambda blk: gqa_attend(blk, k_all, v_all), qb)
    out = o.transpose(1, 0, 2, 3, 4, 5).reshape(B, T, ATT_Q)
    out_c = None
    if need_ctx:
        out_c = gqa_attend(cqh.reshape(B, Lc, ATT_KV_HEADS, ATT_GROUP, ATT_HD), ckh, cvh).reshape(B, Lc, ATT_Q)
    return out, out_c


def dwconv_centred(x, w, b):
    T = x.shape[1]
    xp = jnp.pad(x, ((0, 0), (CONV_LEFT, CONV_W - 1 - CONV_LEFT), (0, 0)))
    out = b
    for j in range(CONV_W):
        out = out + xp[:, j:j + T] * w[j]
    return out


def _lin_combine(left, right):
    a_l, b_l = left
    a_r, b_r = right
    return a_l * a_r, a_r * b_l + b_r


def rglru_scan(x, w_a, b_a, w_i, b_i, lam, h0, reset_first):
    B, T, W = x.shape
    xf = x.astype(jnp.float32)
    xb = xf.reshape(B, T, LRU_BLOCKS, LRU_BS)
    r = jax.nn.sigmoid(jnp.einsum('btnk,nkj->btnj', xb, w_a.astype(jnp.float32)).reshape(B, T, W) + b_a)
    i = jax.nn.sigmoid(jnp.einsum('btnk,nkj->btnj', xb, w_i.astype(jnp.float32)).reshape(B, T, W) + b_i)
    log_a = -LRU_C * r * jax.nn.softplus(-lam.astype(jnp.float32))
    a = jnp.exp(log_a)
    mult = jnp.sqrt(-jnp.expm1(2.0 * log_a))
    if reset_first:
        mult = mult.at[:, 0].set(1.0)
    a_cum, h = lax.associative_scan(_lin_combine, (a, mult * i * xf), axis=1)
    h = h + a_cum * h0[:, None]
    return h.astype(x.dtype), h[:, -1]


def lru_branch(x_lat, y_lat, x_ctx, y_ctx, conv_w, conv_b, w_a, b_a, w_i, b_i, lam, need_ctx):
    xl = dwconv_centred(x_lat, conv_w, conv_b)
    xc = dwconv_centred(x_ctx, conv_w, conv_b)
    h0 = jnp.zeros((x_lat.shape[0], LRU_W), jnp.float32)
    hc_f, s_f = rglru_scan(xc, w_a[0], b_a[0], w_i[0], b_i[0], lam[0], h0, True)
    hc_b, s_b = rglru_scan(flip(xc), w_a[1], b_a[1], w_i[1], b_i[1], lam[1], h0, True)
    hl_f, _ = rglru_scan(xl, w_a[0], b_a[0], w_i[0], b_i[0], lam[0], s_f, False)
    hl_b, _ = rglru_scan(flip(xl), w_a[1], b_a[1], w_i[1], b_i[1], lam[1], s_b, False)
    out = (hl_f + flip(hl_b)) * jax.nn.gelu(y_lat)
    out_c = (hc_f + flip(hc_b)) * jax.nn.gelu(y_ctx) if need_ctx else None
    return out, out_c


def merge_project(o_a, o_b, o_c, gate_logits, b_merge, w_branch, w_out):
    B, T, _ = o_a.shape
    gates = jax.nn.sigmoid(gate_logits.reshape(B, T, N_BRANCH, D_MODEL).astype(jnp.float32) + b_merge)
    proj = jnp.einsum('btnw,nwd->btnd', jnp.stack([o_a, o_b, o_c], axis=2), w_branch)
    merged = jnp.einsum('btnd,btnd->btd', gates.astype(proj.dtype), proj)
    return merged @ w_out


def hybrid_mixer(u, uc, w_in, gla_w_decay, gla_b_decay, gla_norm_g, q_norm_g, k_norm_g,
                 conv_w, conv_b, lru_w_a, lru_b_a, lru_w_i, lru_b_i, lru_lambda,
                 b_merge, w_branch, w_out, cos, sin, need_ctx):
    g_q, g_k, g_v, g_r, g_dec, a_q, a_k, a_v, l_x, l_y, gates = split_cols(u @ w_in)
    cg_q, cg_k, cg_v, cg_r, cg_dec, ca_q, ca_k, ca_v, cl_x, cl_y, cgates = split_cols(uc @ w_in)
    o_gla, oc_gla = gla_branch(g_q, g_k, g_v, g_r, g_dec, cg_q, cg_k, cg_v, cg_r, cg_dec,
                               gla_w_decay, gla_b_decay, gla_norm_g, need_ctx)
    o_att, oc_att = gqa_branch(a_q, a_k, a_v, ca_q, ca_k, ca_v, q_norm_g, k_norm_g, cos, sin, need_ctx)
    o_lru, oc_lru = lru_branch(l_x, l_y, cl_x, cl_y, conv_w, conv_b, lru_w_a, lru_b_a,
                               lru_w_i, lru_b_i, lru_lambda, need_ctx)
    y = merge_project(o_gla, o_att, o_lru, gates, b_merge, w_branch, w_out)
    yc = merge_project(oc_gla, oc_att, oc_lru, cgates, b_merge, w_branch, w_out) if need_ctx else None
    return y, yc


def swiglu(u, w_in, w_out):
    g, up = jnp.split(u @ w_in, 2, axis=-1)
    return (jax.nn.silu(g) * up) @ w_out


def setup_inputs(seed: int = 0) -> dict:
    key = jax.random.key(seed)
    ks = iter(jax.random.split(key, 32))
    f32 = jnp.float32
    D = D_MODEL

    def nrm(shape, scale):
        return jax.random.normal(next(ks), shape, f32) * scale

    u = jax.random.uniform(next(ks), (DEPTH, 2, LRU_W), f32, 0.9, 0.999)
    p = u ** (1.0 / LRU_C)
    return {
        'x': nrm((BATCH, SEQ, D), 1.0),
        'c': nrm((BATCH, D), 1.0),
        'ctx': nrm((BATCH, CTX_LEN, D), 1.0),
        'c_ctx': nrm((D,), 1.0),
        'w_mod': nrm((DEPTH, D, N_MOD * D), 0.5 * D ** -0.5),
        'b_mod': nrm((DEPTH, N_MOD * D), 0.02),
        'norm_mix_g': 1.0 + nrm((DEPTH, D), 0.02),
        'norm_ffn_g': 1.0 + nrm((DEPTH, D), 0.02),
        'w_in': nrm((DEPTH, D, N_IN), D ** -0.5),
        'gla_w_decay': nrm((DEPTH, 2, GLA_RANK, GLA_DK), GLA_RANK ** -0.5),
        'gla_b_decay': nrm((DEPTH, 2, GLA_DK), 0.5),
        'gla_norm_g': 1.0 + nrm((DEPTH, GLA_HDV), 0.02),
        'q_norm_g': 1.0 + nrm((DEPTH, ATT_HD), 0.02),
        'k_norm_g': 1.0 + nrm((DEPTH, ATT_HD), 0.02),
        'conv_w': nrm((DEPTH, CONV_W, LRU_W), CONV_W ** -0.5),
        'conv_b': nrm((DEPTH, LRU_W), 0.02),
        'lru_w_a': nrm((DEPTH, 2, LRU_BLOCKS, LRU_BS, LRU_BS), LRU_BS ** -0.5),
        'lru_b_a': nrm((DEPTH, 2, LRU_W), 0.02),
        'lru_w_i': nrm((DEPTH, 2, LRU_BLOCKS, LRU_BS, LRU_BS), LRU_BS ** -0.5),
        'lru_b_i': nrm((DEPTH, 2, LRU_W), 0.02),
        'lru_lambda': jnp.log(p) - jnp.log1p(-p),
        'b_merge': nrm((DEPTH, N_BRANCH, D), 0.02),
        'w_branch': nrm((DEPTH, N_BRANCH, GLA_DV, D), GLA_DV ** -0.5),
        'w_out': nrm((DEPTH, D, D), D ** -0.5),
        'w_ffn_in': nrm((DEPTH, D, 2 * FFN_HIDDEN), D ** -0.5),
        'w_ffn_out': nrm((DEPTH, FFN_HIDDEN, D), FFN_HIDDEN ** -0.5),
        'final_norm_g': 1.0 + nrm((D,), 0.02),
    }


def reference(x, c, ctx, c_ctx, w_mod, b_mod, norm_mix_g, norm_ffn_g, w_in, gla_w_decay, gla_b_decay,
              gla_norm_g, q_norm_g, k_norm_g, conv_w, conv_b, lru_w_a, lru_b_a, lru_w_i, lru_b_i,
              lru_lambda, b_merge, w_branch, w_out, w_ffn_in, w_ffn_out, final_norm_g):
    T = x.shape[1]
    cos, sin = axial_rope(T)
    sc = jax.nn.silu(c)
    scc = jax.nn.silu(c_ctx)
    h, hc = x, ctx
    for l in range(DEPTH):
        need_ctx = l < DEPTH - 1
        m = sc @ w_mod[l] + b_mod[l]
        mc = scc @ w_mod[l] + b_mod[l]
        sh1, s1, g1, sh2, s2, g2 = [z[:, None] for z in jnp.split(m, N_MOD, axis=-1)]
        csh1, cs1, cg1, csh2, cs2, cg2 = jnp.split(mc, N_MOD, axis=-1)
        u = modulate(rmsnorm(h, norm_mix_g[l]), sh1, s1)
        uc = modulate(rmsnorm(hc, norm_mix_g[l]), csh1, cs1)
        y, yc = hybrid_mixer(u, uc, w_in[l], gla_w_decay[l], gla_b_decay[l], gla_norm_g[l],
                             q_norm_g[l], k_norm_g[l], conv_w[l], conv_b[l], lru_w_a[l], lru_b_a[l],
                             lru_w_i[l], lru_b_i[l], lru_lambda[l], b_merge[l], w_branch[l], w_out[l],
                             cos, sin, need_ctx)
        h = h + g1 * y
        h = h + g2 * swiglu(modulate(rmsnorm(h, norm_ffn_g[l]), sh2, s2), w_ffn_in[l], w_ffn_out[l])
        if need_ctx:
            hc = hc + cg1 * yc
            hc = hc + cg2 * swiglu(modulate(rmsnorm(hc, norm_ffn_g[l]), csh2, cs2), w_ffn_in[l], w_ffn_out[l])
    return rmsnorm(h, final_norm_g)
```

```python
import numpy as np
import concourse.bass as bass
import concourse.mybir as mybir

F32 = mybir.dt.float32
BF16 = mybir.dt.bfloat16
AF = mybir.ActivationFunctionType
ALU = mybir.AluOpType
AX = mybir.AxisListType

SEM_LIMIT = 30000
DT_SIZE = {F32: 4, BF16: 2}


class Buf:
    __slots__ = ("name", "w", "r")

    def __init__(self, name=""):
        self.name = name
        self.w = None
        self.r = []


class Op:
    __slots__ = ("eng", "emit", "sig_ok", "deps", "idx", "need", "sem", "val", "dma")

    def __init__(self, eng, emit, sig_ok, dma):
        self.eng = eng
        self.emit = emit
        self.sig_ok = sig_ok
        self.dma = dma
        self.deps = []
        self.need = False
        self.sem = None
        self.val = 0


ENGS = ("pe", "act", "dve", "sp", "pool")


class Arena:
    def __init__(self, base, limit, banks):
        self.sp = base
        self.limit = limit
        self.banks = banks
        self.rr = 0


class Prog:
    def __init__(self, nc):
        self.nc = nc
        self.ops = {e: [] for e in ENGS}
        self.arena_bytes = 207 * 1024
        self.arena = nc.alloc_sbuf_tensor("arena", [128, self.arena_bytes // 2], BF16)
        self.psum = nc.alloc_psum_tensor("psum", [128, 8, 512], F32)
        self.all_bufs = []
        self.psum_bufs = [self.buf(f"psum{i}") for i in range(8)]
        self.main = Arena(0, self.arena_bytes, list(range(8)))
        self.cur = self.main

    def buf(self, name=""):
        b = Buf(name)
        self.all_bufs.append(b)
        return b

    def mark(self):
        return self.cur.sp

    def release(self, mark):
        self.barrier()
        self.cur.sp = mark

    def split(self, sizes, banks):
        out = []
        off = self.main.sp
        for sz, bk in zip(sizes, banks):
            sz = self.main.limit - off if sz is None else sz
            assert off + sz <= self.main.limit, "sub-arena overflow"
            out.append(Arena(off, off + sz, bk))
            off += sz
        return out

    def interleave(self, gens, weights):
        active = [[g, a, w] for (g, a), w in zip(gens, weights)]
        while active:
            for item in list(active):
                g, a, w = item
                self.cur = a
                try:
                    for _ in range(w):
                        next(g)
                except StopIteration:
                    active.remove(item)
        self.cur = self.main

    def tile(self, shape, dtype, parts=128):
        n = int(np.prod(shape))
        nbytes = n * DT_SIZE[dtype]
        nbytes = (nbytes + 63) // 64 * 64
        off = self.cur.sp
        assert off + nbytes <= self.cur.limit, f"SBUF arena overflow {off}+{nbytes} > {self.cur.limit}"
        self.cur.sp = off + nbytes
        ap = self.arena[0:parts, off // 2:(off + n * DT_SIZE[dtype]) // 2]
        if dtype != BF16:
            ap = ap.bitcast(dtype)
        if len(shape) == 2:
            ap = ap.rearrange("p (a b) -> p a b", b=shape[1])
        elif len(shape) == 3:
            ap = ap.rearrange("p (a b c) -> p a b c", b=shape[1], c=shape[2])
        return ap

    def ps(self, bank=None):
        a = self.cur
        if bank is None:
            bank = a.banks[a.rr % len(a.banks)]
            a.rr += 1
        else:
            bank = a.banks[bank % len(a.banks)]
        return self.psum[:, bank, :], self.psum_bufs[bank]

    def op(self, eng, emit, reads=(), writes=(), sig_ok=True):
        dma = eng in ("sp", "pool")
        o = Op(eng, emit, sig_ok, dma)
        o.idx = len(self.ops[eng])
        deps = []
        for b in reads:
            if b.w is not None:
                deps.append(b.w)
        for b in writes:
            if b.w is not None:
                deps.append(b.w)
            deps.extend(b.r)
        seen = set()
        for d in deps:
            if id(d) in seen:
                continue
            seen.add(id(d))
            if d.eng == eng and not d.dma:
                continue
            o.deps.append(d)
        for b in reads:
            b.r.append(o)
        for b in writes:
            b.w = o
            b.r = []
        self.ops[eng].append(o)
        return o

    def barrier(self):
        last = []
        for e in ENGS:
            for o in reversed(self.ops[e]):
                if o.sig_ok:
                    last.append(o)
                    break
        fence = self.buf("fence")
        for e in ENGS:
            o = Op(e, None, False, e in ("sp", "pool"))
            o.idx = len(self.ops[e])
            o.deps = [d for d in last if not (d.eng == e and not d.dma)]
            self.ops[e].append(o)
        for b in self.all_bufs:
            b.w = None
            b.r = []

    def finalize(self):
        nc = self.nc
        nxt = {}
        for e in ENGS:
            lst = self.ops[e]
            nx = [None] * len(lst)
            cur = None
            for i in range(len(lst) - 1, -1, -1):
                if lst[i].sig_ok:
                    cur = lst[i]
                nx[i] = cur
            nxt[e] = nx
        for e in ENGS:
            for o in self.ops[e]:
                nd = []
                for d in o.deps:
                    t = nxt[d.eng][d.idx]
                    assert t is not None, "dependency on trailing non-signalling op"
                    t.need = True
                    nd.append(t)
                o.deps = nd
        for e in ("sp", "pool"):
            for o in self.ops[e]:
                if o.emit is not None:
                    o.need = True
        self.epochs = {e: [] for e in ENGS}
        nsem = 0
        for e in ENGS:
            cnt = 0
            sem = None
            for o in self.ops[e]:
                if not o.need:
                    continue
                inc = 16 if o.dma else 1
                if sem is None or cnt + inc > SEM_LIMIT:
                    sem = nc.alloc_semaphore(f"s_{e}_{nsem}")
                    nsem += 1
                    cnt = 0
                    self.epochs[e].append([sem, 0])
                cnt += inc
                o.sem = sem
                o.val = cnt
                o.idx = len(self.epochs[e]) - 1
                self.epochs[e][-1][1] = cnt
        self.nsem = nsem
        return self

    def emit_engine(self, e, engobj):
        waited = {}
        n_wait = 0
        for o in self.ops[e]:
            need = {}
            for d in o.deps:
                k = d.sem
                if need.get(k, (None, 0))[1] < d.val:
                    need[k] = (d.sem, d.val)
                if d.dma:
                    for (osem, ofin) in self.epochs[d.eng][:d.idx]:
                        if need.get(osem, (None, 0))[1] < ofin:
                            need[osem] = (osem, ofin)
            for k, (sem, val) in need.items():
                if waited.get(k, 0) >= val:
                    continue
                engobj.wait_ge(sem, val)
                waited[k] = val
                n_wait += 1
            if o.emit is None:
                continue
            ins = o.emit(engobj)
            if o.need:
                ins.then_inc(o.sem, 16 if o.dma else 1)
        return n_wait

    def emit_all(self):
        nc = self.nc
        self.finalize()
        stats = {}
        with nc.Block() as block:
            @block.tensor
            def _(t):
                stats["pe"] = self.emit_engine("pe", t)

            @block.scalar
            def _(s):
                stats["act"] = self.emit_engine("act", s)

            @block.vector
            def _(v):
                stats["dve"] = self.emit_engine("dve", v)

            @block.sync
            def _(s):
                stats["sp"] = self.emit_engine("sp", s)

            @block.gpsimd
            def _(g):
                stats["pool"] = self.emit_engine("pool", g)
        return stats

    def dma(self, out, in_, reads=(), writes=(), q="sp"):
        return self.op(q, lambda e: e.dma_start(out=out, in_=in_), reads, writes)

    def mm(self, out, lhsT, rhs, start, stop, reads=(), writes=()):
        return self.op("pe", lambda e: e.matmul(out, lhsT, rhs, start=start, stop=stop),
                       reads, writes, sig_ok=bool(stop))

    def transpose(self, out, in_, ident, reads=(), writes=()):
        return self.op("pe", lambda e: e.transpose(out, in_, ident), reads, writes)

    def act(self, out, in_, func, reads=(), writes=(), scale=1.0, bias=None, accum_out=None):
        kw = {}
        if bias is not None:
            kw["bias"] = bias
        if accum_out is not None:
            kw["accum_out"] = accum_out
        return self.op("act", lambda e: e.activation(out=out, in_=in_, func=func, scale=scale, **kw),
                       reads, writes)

    def tt(self, out, in0, in1, op, reads=(), writes=(), eng="dve"):
        return self.op(eng, lambda e: e.tensor_tensor(out, in0, in1, op), reads, writes)

    def ts(self, out, in0, s1, s2, op0, op1=None, reads=(), writes=(), eng="dve"):
        if op1 is None:
            return self.op(eng, lambda e: e.tensor_scalar(out, in0, s1, None, op0), reads, writes)
        return self.op(eng, lambda e: e.tensor_scalar(out, in0, s1, s2, op0, op1), reads, writes)

    def stt(self, out, in0, scalar, in1, op0, op1, reads=(), writes=()):
        return self.op("dve", lambda e: e.scalar_tensor_tensor(out, in0, scalar, in1, op0, op1),
                       reads, writes)

    def copy(self, out, in_, reads=(), writes=(), eng="dve"):
        if eng == "act":
            return self.op("act", lambda e: e.activation(out=out, in_=in_, func=AF.Copy), reads, writes)
        return self.op(eng, lambda e: e.tensor_copy(out, in_), reads, writes)

    def memset(self, ap, val, writes=(), eng="dve"):
        return self.op(eng, lambda e: e.memset(ap, val), (), writes)
from concourse.bass_utils import run_bass_kernel_spmd

D = 2048
KC = 16
TL = 2048
TCX = 256
T = TL + TCX
DEPTH = 4
NMOD = 6
EPS = 1e-6
FFN_H = 5632
N_IN = 19488
O_GQ, O_GK, O_GV, O_GR, O_GDEC, O_AQ, O_AK, O_AV, O_LX, O_LY, O_GATES = (
    0, 1024, 2048, 4096, 6144, 6176, 8224, 8736, 9248, 11296, 13344)
GELU_C = 1.5957691216057308
INTERLEAVE_GQA_LRU = False

TT512 = [(0, 512), (512, 512), (1024, 512), (1536, 512), (2048, 256)]
TT128 = [(i * 128, 128) for i in range(18)]

V_BMOD = 0
V_GMIX = 96
V_GFFN = 112
V_BDEC = 128
V_CONVW = 144
V_CONVB = 208
V_LBA = 224
V_LBI = 256
V_LAM = 288
V_BMRG = 320
V_PER_LAYER = 368
V_FINALG = DEPTH * V_PER_LAYER
V_C = V_FINALG + 16
V_CCTX = V_C + 16
NV = V_CCTX + 16


def is_ctx(t0):
    return t0 >= TL


class K:
    pass


def build_program(n_layers=DEPTH, debug=None, stop_after=None, skip=None, final=True):
    debug = debug or set()
    nc = bass.Bass("TRN2", target_bir_lowering=False)
    P = Prog(nc)
    k = K()
    k.nc, k.P = nc, P

    def dram_in(name, shape, dt=F32):
        return nc.dram_tensor(name, list(shape), dt, kind="ExternalInput").ap()

    def scratch(name, shape, dt):
        kind = "ExternalOutput" if name in debug else "Internal"
        return nc.dram_tensor(name, list(shape), dt, kind=kind).ap()

    k.hT0 = dram_in("hT0", [KC, 128, T])
    k.vecs_d = dram_in("vecs", [128, NV])
    k.gbc_d = dram_in("gbc", [DEPTH, 128, 512 + 128 + 128])
    k.rope_d = dram_in("rope", [128, 16, 128])
    k.consts_d = dram_in("consts", [128, 128 + 64 + 64 + 64])
    k.w_mod = dram_in("w_mod", [n_layers, D, NMOD * D])
    k.w_in = dram_in("w_in", [n_layers, D, N_IN])
    k.wdec = dram_in("wdec", [n_layers, 2, 32, 1024])
    k.lru_w = dram_in("lru_w", [n_layers, 2, 2, 16, 128, 128])
    k.w_branch = dram_in("w_branch", [n_layers, 3, D, D])
    k.w_out = dram_in("w_out", [n_layers, D, D])
    k.w_ffn_in = dram_in("w_ffn_in", [n_layers, D, 2 * FFN_H])
    k.w_ffn_out = dram_in("w_ffn_out", [n_layers, FFN_H, D])
    k.outT = nc.dram_tensor("outT", [KC, 128, TL], F32, kind="ExternalOutput").ap()

    k.hT = scratch("hT", [KC, 128, T], F32)
    k.qgT = scratch("qgT", [8, 128, T], BF16)
    k.kgT = scratch("kgT", [8, 128, T], BF16)
    k.vg = scratch("vg", [T, 2048], BF16)
    k.rg = scratch("rg", [T, 2048], F32)
    k.decT = scratch("decT", [32, T], BF16)
    k.qaT = scratch("qaT", [16, 128, T], BF16)
    k.kaT = scratch("kaT", [4, 128, T], BF16)
    k.va = scratch("va", [T, 512], BF16)
    k.lxT = scratch("lxT", [16, 128, T], F32)
    k.lyT = scratch("lyT", [16, 128, T], F32)
    k.gatesT = scratch("gatesT", [48, 128, T], F32)
    k.ogf = scratch("ogf", [T, 2048], F32)
    k.ogb = scratch("ogb", [T, 2048], F32)
    k.oaT = scratch("oaT", [16, 128, T], BF16)
    k.obT = scratch("obT", [16, 128, T], BF16)
    k.ocT = scratch("ocT", [16, 128, T], BF16)
    k.dbg_uT = scratch("dbg_uT", [KC, 128, T], BF16) if "dbg_uT" in debug else None
    k.dbg_lru = scratch("dbg_lru", [7, 128, T], F32) if "dbg_lru" in debug else None
    k.dbg_mcol = scratch("dbg_mcol", [128, 96 * 2], F32) if "dbg_mcol" in debug else None

    k.vecs = P.tile([NV], F32)
    k.b_vecs = P.buf("vecs")
    k.consts = P.tile([320], F32)
    k.b_consts = P.buf("consts")
    k.ident_bf = P.tile([128], BF16)
    k.ones_bf = P.tile([128], BF16)
    k.ones_f = P.tile([128], F32)
    k.sT = P.tile([16, 2], BF16)
    k.mcol = P.tile([96, 2], F32)
    k.b_mcol = P.buf("mcol")
    k.modA = P.tile([2, 2, 16], F32)
    k.b_modA = P.buf("modA")

    P.dma(k.vecs, k.vecs_d, writes=[k.b_vecs])
    P.dma(k.consts, k.consts_d, writes=[k.b_consts])
    P.copy(k.ident_bf, k.consts[:, 0:128], reads=[k.b_consts], writes=[k.b_consts])
    P.memset(k.ones_bf, 1.0, writes=[k.b_consts])
    P.memset(k.ones_f, 1.0, writes=[k.b_consts])
    P.act(k.sT[:, :, 0], k.vecs[:, V_C:V_C + 16], AF.Silu, reads=[k.b_vecs], writes=[k.b_mcol])
    P.act(k.sT[:, :, 1], k.vecs[:, V_CCTX:V_CCTX + 16], AF.Silu, reads=[k.b_vecs], writes=[k.b_mcol])
    k.b_hT = P.buf("hT")
    for c in range(KC):
        P.dma(k.hT[c], k.hT0[c], writes=[k.b_hT])

    base_mark = P.mark()
    stages = ["mod", "norm1", "win", "gla", "gqa", "lru", "merge", "ffn"]
    upto = stages.index(stop_after) if stop_after else len(stages)
    skip = skip or set()
    for l in range(n_layers):
        need_ctx = (l < DEPTH - 1)
        alloc_wpool(k, 3)
        phase_mod(k, l)
        P.release(base_mark)
        if upto >= 1:
            k.uT = P.tile([KC, T], BF16)
            k.b_uT = [[P.buf(f"uT{c}_{i}") for i in range(len(TT512))] for c in range(KC)]
            alloc_wpool(k, 3)
            tiles = [(i * 256, 256) for i in range(T // 256)]
            norm_core(k, l, 0, tiles, dst=lambda c, t0, tn: k.uT[:, c, t0:t0 + tn],
                      dst_buf=lambda c, t0: k.b_uT[c][min(t0 // 512, 4)])
            if upto >= 2 and "win" not in skip:
                phase_win(k, l)
            if "dbg_uT" in debug and l == n_layers - 1:
                P.barrier()
                for c in range(KC):
                    P.dma(k.dbg_uT[c], k.uT[:, c, :])
            P.release(base_mark)
        if upto >= 3 and "gla" not in skip:
            phase_gla(k, l, need_ctx)
        if INTERLEAVE_GQA_LRU and upto >= 5 and "gqa" not in skip and "lru" not in skip:
            a_gqa, a_lru = P.split([68 * 1024, None], [[0, 1, 2, 3, 4, 5], [6, 7]])
            P.interleave([(phase_gqa(k, l, need_ctx), a_gqa), (phase_lru(k, l, need_ctx), a_lru)], [5, 2])
            P.barrier()
        else:
            if upto >= 4 and "gqa" not in skip:
                for _ in phase_gqa(k, l, need_ctx):
                    pass
            if upto >= 5 and "lru" not in skip:
                for _ in phase_lru(k, l, need_ctx):
                    pass
        if upto >= 6 and "merge" not in skip:
            alloc_wpool(k, 4)
            phase_merge(k, l, need_ctx)
            P.release(base_mark)
        if upto >= 7 and "ffn" not in skip:
            alloc_wpool(k, 4)
            phase_ffn(k, l, need_ctx)
            P.release(base_mark)

    if k.dbg_mcol is not None:
        P.barrier()
        P.dma(k.dbg_mcol, k.mcol.rearrange("p a b -> p (a b)"))
    P.barrier()
    if final:
        phase_final(k)
    else:
        fin = P.tile([TL], F32)
        for c in range(KC):
            P.dma(fin, k.hT[c, :, 0:TL])
            P.dma(k.outT[c], fin)
    P.barrier()
    stats = P.emit_all()
    return nc, stats


def alloc_wpool(k, n):
    k.wpool = [(k.P.tile([8192], BF16), k.P.buf(f"w{i}")) for i in range(n)]
    k.wrr = 0


def wbuf(k):
    w, b = k.wpool[k.wrr]
    k.wrr = (k.wrr + 1) % len(k.wpool)
    return w, b


def load_w(k, w_dram, kc, c0, ncols):
    P = k.P
    w, b = wbuf(k)
    view = w[:, 0:kc * ncols].rearrange("p (a n) -> p a n", n=ncols)
    src = w_dram[:, c0:c0 + ncols].rearrange("(a p) n -> p a n", p=128)
    P.dma(view, src, writes=[b], q="pool")
    return view, b


def phase_mod(k, l):
    P = k.P
    vb = l * V_PER_LAYER
    ps, pb = P.ps()
    for blk in range(24):
        w, wb_ = load_w(k, k.w_mod[l], KC, blk * 512, 512)
        for j in range(4):
            n = blk * 4 + j
            for kc in range(KC):
                P.mm(ps[:, 2 * n:2 * n + 2], w[:, kc, j * 128:(j + 1) * 128], k.sT[:, kc, :],
                     start=(kc == 0), stop=(kc == KC - 1),
                     reads=[wb_, k.b_mcol], writes=[pb])
    bm = k.vecs[:, vb + V_BMOD:vb + V_BMOD + 96]
    P.tt(k.mcol, ps[:, 0:192].rearrange("p (a b) -> p a b", b=2),
         bm.unsqueeze(2).to_broadcast([128, 96, 2]), ALU.add,
         reads=[pb, k.b_vecs], writes=[k.b_mcol])
    for n, (which, vg) in enumerate(((1, V_GMIX), (4, V_GFFN))):
        for s in range(2):
            P.stt(k.modA[:, n, s, :], k.mcol[:, which * 16:(which + 1) * 16, s], 1.0,
                  k.vecs[:, vb + vg:vb + vg + 16], ALU.add, ALU.mult,
                  reads=[k.b_mcol, k.b_vecs], writes=[k.b_modA])


def linear_fm(k, w_dram, c0, ncols, epilogue, tiles=TT512, kc_n=KC, rhs=None, blk=512):
    P = k.P
    for b0 in range(0, ncols, blk):
        nb = min(blk, ncols - b0)
        w, wb_ = load_w(k, w_dram, kc_n, c0 + b0, nb)
        for j0 in range(0, nb, 128):
            m = min(128, nb - j0)
            for ti, (t0, tn) in enumerate(tiles):
                ps, pb = P.ps()
                for kc in range(kc_n):
                    P.mm(ps[0:m, 0:tn], w[:, kc, j0:j0 + m], k.uT[:, kc, t0:t0 + tn],
                         start=(kc == 0), stop=(kc == kc_n - 1),
                         reads=[wb_, k.b_uT[kc][ti]], writes=[pb])
                epilogue((b0 + j0) // 128, ti, t0, tn, ps, pb)


def linear_tm(k, w_dram, c0, ncols, epilogue, tiles=TT128, blk=512):
    P = k.P
    for bi, b0 in enumerate(range(0, ncols, blk)):
        nb = min(blk, ncols - b0)
        w, wb_ = load_w(k, w_dram, KC, c0 + b0, nb)
        for ti, (t0, tn) in enumerate(tiles):
            t5 = [j for j, (a, b) in enumerate(TT512) if a <= t0 < a + b][0]
            ps, pb = P.ps()
            for kc in range(KC):
                P.mm(ps[0:tn, 0:nb], k.uT[:, kc, t0:t0 + tn], w[:, kc, 0:nb],
                     start=(kc == 0), stop=(kc == KC - 1),
                     reads=[wb_, k.b_uT[kc][t5]], writes=[pb])
            epilogue(bi, ti, t0, ps, pb)


def phase_win(k, l):
    P = k.P
    vb = l * V_PER_LAYER
    W = k.w_in[l]
    mk = P.mark()
    st_bf = [(P.tile([T], BF16), P.buf(f"stbf{i}")) for i in range(2)]
    st_f = [(P.tile([T], F32), P.buf(f"stf{i}")) for i in range(2)]
    st_tm_bf = [(P.tile([512], BF16), P.buf(f"sttb{i}")) for i in range(3)]
    st_tm_f = [(P.tile([512], F32), P.buf(f"sttf{i}")) for i in range(3)]
    tmp_f = [(P.tile([512], F32), P.buf(f"tmpf{i}")) for i in range(3)]
    rr = {"bf": 0, "f": 0, "tb": 0, "tf": 0, "tmp": 0}

    def nxt(pool, key):
        i = rr[key]
        rr[key] = (i + 1) % len(pool)
        return pool[i]

    def fm_group(c0, nchunks, out_dram, dt, fn):
        cur = {}

        def epi(j, ti, t0, tn, ps, pb):
            if ti == 0:
                cur["st"] = nxt(st_bf, "bf") if dt == BF16 else nxt(st_f, "f")
            st, sb = cur["st"]
            fn(j, t0, tn, ps, pb, st[:, t0:t0 + tn], sb)
            if ti == len(TT512) - 1:
                P.dma(out_dram[j], st, reads=[sb])
        linear_fm(k, W, c0, nchunks * 128, epi)

    fm_group(O_GQ, 8, k.qgT, BF16,
             lambda j, t0, tn, ps, pb, dst, sb: P.act(dst, ps[:, 0:tn], AF.Copy, scale=1.0 / 16.0,
                                                      reads=[pb], writes=[sb]))
    fm_group(O_GK, 8, k.kgT, BF16,
             lambda j, t0, tn, ps, pb, dst, sb: P.copy(dst, ps[:, 0:tn], reads=[pb], writes=[sb]))

    fm_group(O_LX, 16, k.lxT, F32,
             lambda j, t0, tn, ps, pb, dst, sb: P.copy(dst, ps[:, 0:tn], reads=[pb], writes=[sb], eng="act"))

    def epi_ly(j, t0, tn, ps, pb, dst, sb):
        tm, tb = nxt(tmp_f, "tmp")
        P.act(tm[:, 0:tn], ps[:, 0:tn], AF.Square, reads=[pb], writes=[tb])
        P.ts(tm[:, 0:tn], tm[:, 0:tn], 0.044715, 1.0, ALU.mult, ALU.add, reads=[tb], writes=[tb])
        P.tt(tm[:, 0:tn], tm[:, 0:tn], ps[:, 0:tn], ALU.mult, reads=[tb, pb], writes=[tb])
        P.act(tm[:, 0:tn], tm[:, 0:tn], AF.Sigmoid, scale=GELU_C, reads=[tb], writes=[tb])
        P.tt(dst, tm[:, 0:tn], ps[:, 0:tn], ALU.mult, reads=[tb, pb], writes=[sb])
    fm_group(O_LY, 16, k.lyT, F32, epi_ly)

    def epi_gate(j, t0, tn, ps, pb, dst, sb):
        P.act(dst, ps[:, 0:tn], AF.Sigmoid, bias=k.vecs[:, vb + V_BMRG + j:vb + V_BMRG + j + 1],
              reads=[pb, k.b_vecs], writes=[sb])
    fm_group(O_GATES, 48, k.gatesT, F32, epi_gate)

    def epi_dec(j, ti, t0, tn, ps, pb):
        if ti == 0:
            epi_dec.st = nxt(st_bf, "bf")
        st, sb = epi_dec.st
        P.copy(st[0:32, t0:t0 + tn], ps[0:32, 0:tn], reads=[pb], writes=[sb])
        if ti == len(TT512) - 1:
            P.dma(k.decT, st[0:32, :], reads=[sb])
    linear_fm(k, W, O_GDEC, 32, epi_dec)

    def mk_epi_tm_copy(out_dram, ocol0):
        def epi(bi, ti, t0, ps, pb):
            st, sb = nxt(st_tm_bf, "tb")
            P.copy(st, ps, reads=[pb], writes=[sb], eng=("act" if ti % 2 else "dve"))
            P.dma(out_dram[t0:t0 + 128, ocol0 + bi * 512:ocol0 + (bi + 1) * 512], st, reads=[sb])
        return epi
    linear_tm(k, W, O_GV, 2048, mk_epi_tm_copy(k.vg, 0))
    linear_tm(k, W, O_AV, 512, mk_epi_tm_copy(k.va, 0))

    def epi_r(bi, ti, t0, ps, pb):
        st, sb = nxt(st_tm_f, "tf")
        P.act(st, ps, AF.Silu, reads=[pb], writes=[sb])
        P.dma(k.rg[t0:t0 + 128, bi * 512:(bi + 1) * 512], st, reads=[sb])
    linear_tm(k, W, O_GR, 2048, epi_r)

    P.release(mk)
    gbc = P.tile([768], F32)
    b_gbc = P.buf("gbc")
    P.dma(gbc, k.gbc_d[l], writes=[b_gbc])
    rope = P.tile([16, 128], F32)
    b_rope = P.buf("rope")
    P.dma(rope, k.rope_d, writes=[b_rope])
    hst = P.tile([4, T], BF16)
    b_hst = P.buf("hst")
    xn_pool = [(P.tile([512], F32), P.buf(f"xn{i}")) for i in range(2)]
    xr_pool = [(P.tile([512], BF16), P.buf(f"xr{i}")) for i in range(2)]
    ss_pool = [(P.tile([8], F32), P.buf(f"ss{i}")) for i in range(2)]
    rt_pool = [(P.tile([4, 256], F32), P.buf(f"rt{i}")) for i in range(2)]
    cnt = {"i": 0}

    def mk_epi_qk(goff, out_dram, head0):
        g_b = gbc[:, goff:goff + 128]

        def epi(bi, ti, t0, ps, pb):
            i = cnt["i"]
            cnt["i"] += 1
            xn, xb = xn_pool[i % 2]
            xr, xrb = xr_pool[i % 2]
            ss, ssb = ss_pool[i % 2]
            rt, rtb = rt_pool[i % 2]
            for h in range(4):
                P.act(xn[:, h * 128:(h + 1) * 128], ps[:, h * 128:(h + 1) * 128], AF.Square,
                      accum_out=ss[:, h:h + 1], reads=[pb], writes=[xb, ssb])
            P.ts(ss[:, 0:4], ss[:, 0:4], 1.0 / 128.0, EPS, ALU.mult, ALU.add, reads=[ssb], writes=[ssb])
            P.act(ss[:, 0:4], ss[:, 0:4], AF.Sqrt, reads=[ssb], writes=[ssb])
            P.op("dve", lambda e: e.reciprocal(ss[:, 0:4], ss[:, 0:4]), [ssb], [ssb])
            xn3 = xn.rearrange("p (h d) -> p h d", d=128)
            ps3 = ps.rearrange("p (h d) -> p h d", d=128)
            P.tt(xn3, ps3, ss[:, 0:4].unsqueeze(2).to_broadcast([128, 4, 128]), ALU.mult,
                 reads=[pb, ssb], writes=[xb])
            xr3 = xr.rearrange("p (h d) -> p h d", d=128)
            if is_ctx(t0):
                P.tt(xr3, xn3, g_b.unsqueeze(1).to_broadcast([128, 4, 128]), ALU.mult,
                     reads=[xb, b_gbc], writes=[xrb])
            else:
                P.tt(xn3, xn3, g_b.unsqueeze(1).to_broadcast([128, 4, 128]), ALU.mult,
                     reads=[xb, b_gbc], writes=[xb])
                x4 = xn.rearrange("p (h d two) -> p h d two", d=64, two=2)
                o4 = xr.rearrange("p (h d two) -> p h d two", d=64, two=2)
                x1, x2 = x4[:, :, :, 0], x4[:, :, :, 1]
                cs = rope[:, ti, 0:64].unsqueeze(1).to_broadcast([128, 4, 64])
                sn = rope[:, ti, 64:128].unsqueeze(1).to_broadcast([128, 4, 64])
                rt3 = rt
                ta, tb_ = rt3[:, 0:4, 0:64], rt3[:, 0:4, 64:128]
                tc, td = rt3[:, 0:4, 128:192], rt3[:, 0:4, 192:256]
                P.tt(ta, x1, cs, ALU.mult, reads=[xb, b_rope], writes=[rtb])
                P.tt(tb_, x2, sn, ALU.mult, reads=[xb, b_rope], writes=[rtb])
                P.tt(tc, x1, sn, ALU.mult, reads=[xb, b_rope], writes=[rtb])
                P.tt(td, x2, cs, ALU.mult, reads=[xb, b_rope], writes=[rtb])
                P.tt(o4[:, :, :, 0], ta, tb_, ALU.subtract, reads=[rtb], writes=[xrb])
                P.tt(o4[:, :, :, 1], tc, td, ALU.add, reads=[rtb], writes=[xrb])
            pt, ptb = P.ps()
            ptv = pt.bitcast(BF16)
            for h in range(4):
                P.transpose(ptv[:, h * 128:(h + 1) * 128], xr[:, h * 128:(h + 1) * 128], k.ident_bf,
                            reads=[xrb, k.b_consts], writes=[ptb])
            P.copy(hst[:, :, t0:t0 + 128], ptv[:, 0:512].rearrange("p (h t) -> p h t", t=128),
                   reads=[ptb], writes=[b_hst], eng="act")
            if ti == len(TT128) - 1:
                for h in range(4):
                    P.dma(out_dram[head0 + bi * 4 + h], hst[:, h, :], reads=[b_hst])
        return epi
    linear_tm(k, W, O_AQ, 2048, mk_epi_qk(512, k.qaT, 0))
    linear_tm(k, W, O_AK, 512, mk_epi_qk(640, k.kaT, 0))
    P.release(mk)
ST768 = [[(0, 512), (512, 256)], [(768, 512), (1280, 256)], [(1536, 512), (2048, 256)]]


def phase_gqa(k, l, need_ctx):
    P = k.P
    mk = P.mark()
    kT = P.tile([4, T], BF16)
    b_kT = P.buf("kT")
    vv = P.tile([18, 512], BF16)
    b_vv = P.buf("vv")
    P.dma(kT, k.kaT.rearrange("h p t -> p h t"), writes=[b_kT])
    P.dma(vv, k.va.rearrange("(c p) n -> p c n", p=128), writes=[b_vv])
    qpool = [(P.tile([T], BF16), P.buf(f"q{i}")) for i in range(2)]
    opool = [(P.tile([T], BF16), P.buf(f"o{i}")) for i in range(2)]
    ppool = [(P.tile([512], BF16), P.buf(f"pT{i}")) for i in range(4)]
    rpool = [(P.tile([512], F32), P.buf(f"rd{i}")) for i in range(2)]
    scale = 128.0 ** -0.5
    step = 0
    qt = 0

    def loadq(h):
        q, qb = qpool[h % 2]
        P.dma(q, k.qaT[h], writes=[qb])

    loadq(0)
    for h in range(16):
        if h + 1 < 16:
            loadq(h + 1)
        q, qb = qpool[h % 2]
        o, ob = opool[h % 2]
        kvh = h // 4
        tiles = TT512 if need_ctx else TT512[:4]
        for (t0, tn) in tiles:
            chunks = [16, 17] if is_ctx(t0) else list(range(18))
            nb = len(P.cur.banks)
            if nb >= 8:
                acc_o, b_o = P.ps(bank=qt % 2)
                acc_d, b_d = P.ps(bank=2 + qt % 2)
                sc0, nsc = 4, 4
            else:
                acc_o, b_o = P.ps(bank=qt % 2)
                acc_d, b_d = P.ps(bank=2 + qt % 2)
                sc0, nsc = 4, nb - 4
            qt += 1

            def score(sc, step):
                ps, pb = P.ps(bank=sc0 + step % nsc)
                P.mm(ps[:, 0:tn], kT[:, kvh, sc * 128:(sc + 1) * 128], q[:, t0:t0 + tn], True, True,
                     reads=[b_kT, qb], writes=[pb])
                return ps, pb
            pend = score(chunks[0], step)
            for ci, sc in enumerate(chunks):
                ps, pb = pend
                if ci + 1 < len(chunks):
                    pend = score(chunks[ci + 1], step + 1)
                pT, ptb = ppool[step % 4]
                P.act(pT[:, 0:tn], ps[:, 0:tn], AF.Exp, scale=scale, reads=[pb], writes=[ptb])
                first, lastc = ci == 0, ci == len(chunks) - 1
                P.mm(acc_o[:, 0:tn], vv[:, sc, kvh * 128:(kvh + 1) * 128], pT[:, 0:tn], first, lastc,
                     reads=[b_vv, ptb], writes=[b_o])
                P.mm(acc_d[:, 0:tn], k.ones_bf, pT[:, 0:tn], first, lastc,
                     reads=[k.b_consts, ptb], writes=[b_d])
                step += 1
            rd, rdb = rpool[qt % 2]
            P.op("dve", lambda e, rd=rd, acc_d=acc_d, tn=tn: e.reciprocal(rd[:, 0:tn], acc_d[:, 0:tn]),
                 [b_d], [rdb])
            P.tt(o[:, t0:t0 + tn], acc_o[:, 0:tn], rd[:, 0:tn], ALU.mult, reads=[b_o, rdb], writes=[ob])
            yield
        te = T if need_ctx else TL
        P.dma(k.obT[h][:, 0:te], o[:, 0:te], reads=[ob])
    P.release(mk)


def phase_lru(k, l, need_ctx):
    P = k.P
    HALF = T // 2

    def rev_copy(dst, src, reads, writes):
        P.copy(dst[:, HALF:T], src[:, 0:HALF][:, ::-1], reads=reads, writes=writes)
        P.copy(dst[:, 0:HALF], src[:, HALF:T][:, ::-1], reads=reads, writes=writes)

    vb = l * V_PER_LAYER
    mk = P.mark()
    lw = P.tile([64, 128], BF16)
    b_lw = P.buf("lw")
    for ai in range(2):
        for dr in range(2):
            g = ai * 2 + dr
            P.dma(lw[:, g * 16:(g + 1) * 16, :], k.lru_w[l, ai, dr].rearrange("n k j -> k n j"),
                  writes=[b_lw], q="pool")
    coef = P.tile([32], F32)
    b_coef = P.buf("coef")
    lam = k.vecs[:, vb + V_LAM:vb + V_LAM + 32]
    P.act(coef, lam, AF.Exp, scale=-1.0, reads=[k.b_vecs], writes=[b_coef])
    P.act(coef, coef, AF.Ln, bias=1.0, reads=[b_coef], writes=[b_coef])
    P.ts(coef, coef, -8.0, None, ALU.mult, reads=[b_coef], writes=[b_coef])
    X = P.tile([T], F32); bX = P.buf("X")
    XC = P.tile([T], F32); bXC = P.buf("XC")
    XB = P.tile([T], BF16); bXB = P.buf("XB")
    A = P.tile([T], F32); bA = P.buf("A")
    AR = P.tile([T], F32); bAR = P.buf("AR")
    HR = P.tile([T], F32); bHR = P.buf("HR")
    XCR = P.tile([T], F32); bXCR = P.buf("XCR")
    Hd = [P.tile([T], F32) for _ in range(2)]; bH = [P.buf("Hf"), P.buf("Hb")]
    GY = P.tile([T], F32); bGY = P.buf("GY")
    OC = P.tile([T], BF16); bOC = P.buf("OC")
    segs = [(0, TL), (TL, T)]
    for c in range(16):
        P.dma(X, k.lxT[c], writes=[bX])
        P.dma(GY, k.lyT[c], writes=[bGY])
        cw = lambda j: k.vecs[:, vb + V_CONVW + j * 16 + c:vb + V_CONVW + j * 16 + c + 1]
        cb = k.vecs[:, vb + V_CONVB + c:vb + V_CONVB + c + 1]
        P.act(XC, X, AF.Identity, scale=cw(2), bias=cb, reads=[bX, k.b_vecs], writes=[bXC])
        for (s0, s1) in segs:
            for j, off in ((0, -2), (1, -1), (3, 1)):
                lo, hi = max(s0, s0 - off), min(s1, s1 - off)
                P.stt(XC[:, lo:hi], X[:, lo + off:hi + off], cw(j), XC[:, lo:hi], ALU.mult, ALU.add,
                      reads=[bX, k.b_vecs, bXC], writes=[bXC])
        P.copy(XB, XC, reads=[bXC], writes=[bXB], eng="act")
        for dr in range(2):
            H, bHd = Hd[dr], bH[dr]
            Mt, bM = X, bX
            wa = lw[:, (0 * 2 + dr) * 16 + c, :]
            wi = lw[:, (1 * 2 + dr) * 16 + c, :]
            ba = k.vecs[:, vb + V_LBA + dr * 16 + c:vb + V_LBA + dr * 16 + c + 1]
            bi = k.vecs[:, vb + V_LBI + dr * 16 + c:vb + V_LBI + dr * 16 + c + 1]
            cf = coef[:, dr * 16 + c:dr * 16 + c + 1]
            for (t0, tn) in TT512:
                ps, pb = P.ps()
                P.mm(ps[:, 0:tn], wa, XB[:, t0:t0 + tn], True, True, reads=[b_lw, bXB], writes=[pb])
                P.act(A[:, t0:t0 + tn], ps[:, 0:tn], AF.Sigmoid, bias=ba, reads=[pb, k.b_vecs], writes=[bA])
                ps2, pb2 = P.ps()
                P.mm(ps2[:, 0:tn], wi, XB[:, t0:t0 + tn], True, True, reads=[b_lw, bXB], writes=[pb2])
                P.act(H[:, t0:t0 + tn], ps2[:, 0:tn], AF.Sigmoid, bias=bi, reads=[pb2, k.b_vecs], writes=[bHd])
            if dr == 0:
                AA, bAA = AR, bAR
                P.copy(AR[:, 0:TCX], A[:, TL:T], reads=[bA], writes=[bAR])
                P.copy(AR[:, TCX:T], A[:, 0:TL], reads=[bA], writes=[bAR])
                P.act(AR, AR, AF.Exp, scale=cf, reads=[bAR, b_coef], writes=[bAR])
                P.copy(HR[:, 0:TCX], H[:, TL:T], reads=[bHd], writes=[bHR])
                P.copy(HR[:, TCX:T], H[:, 0:TL], reads=[bHd], writes=[bHR])
                P.copy(XCR[:, 0:TCX], XC[:, TL:T], reads=[bXC], writes=[bXCR])
                P.copy(XCR[:, TCX:T], XC[:, 0:TL], reads=[bXC], writes=[bXCR])
                Hsrc, XCsrc = HR, XCR
                first = 0
            else:
                AA, bAA = AR, bAR
                rev_copy(AR, A, [bA], [bAR])
                P.act(AR, AR, AF.Exp, scale=cf, reads=[bAR, b_coef], writes=[bAR])
                rev_copy(HR, H, [bHd], [bHR])
                rev_copy(XCR, XC, [bXC], [bXCR])
                Hsrc, XCsrc = HR, XCR
                first = 0
            P.tt(Mt, AA, AA, ALU.mult, reads=[bAA], writes=[bM])
            P.act(Mt, Mt, AF.Sqrt, scale=-1.0, bias=1.0, reads=[bM], writes=[bM])
            P.tt(Mt, Mt, Hsrc, ALU.mult, reads=[bM, bHd, bHR], writes=[bM])
            P.tt(Mt, Mt, XCsrc, ALU.mult, reads=[bM, bXC, bXCR], writes=[bM])
            P.act(Mt[:, 0:1], Hsrc[:, 0:1], AF.Identity, scale=XCsrc[:, 0:1],
                  reads=[bM, bHd, bHR, bXC, bXCR], writes=[bM])
            if getattr(k, "dbg_lru", None) is not None and c == 0 and l == 0 and dr == 1:
                P.dma(k.dbg_lru[4], AR, reads=[bAR])
                P.dma(k.dbg_lru[5], Mt, reads=[bM])
            P.op("dve", lambda e, H=H, AR=AR, Mt=Mt: e.tensor_tensor_scan(
                H, AR, Mt, 0.0, ALU.mult, ALU.add), [bAR, bM], [bHd])
            if getattr(k, "dbg_lru", None) is not None and c == 0 and l == 0 and dr == 1:
                P.dma(k.dbg_lru[6], H, reads=[bHd])
            if dr == 0:
                P.copy(HR[:, 0:TL], H[:, TCX:T], reads=[bHd], writes=[bHR])
                P.copy(HR[:, TL:T], H[:, 0:TCX], reads=[bHd], writes=[bHR])
                P.copy(H, HR, reads=[bHR], writes=[bHd])
            yield
        te = T if need_ctx else TL
        rev_copy(HR, Hd[1], [bH[1]], [bHR])
        if getattr(k, "dbg_lru", None) is not None and c == 0 and l == 0:
            P.dma(k.dbg_lru[0], Hd[0], reads=[bH[0]])
            P.dma(k.dbg_lru[1], HR, reads=[bHR])
            P.dma(k.dbg_lru[2], XC, reads=[bXC])
            P.dma(k.dbg_lru[3], GY, reads=[bGY])
        P.tt(Hd[0][:, 0:te], Hd[0][:, 0:te], HR[:, 0:te], ALU.add, reads=[bH[0], bHR], writes=[bH[0]])
        P.tt(OC[:, 0:te], Hd[0][:, 0:te], GY[:, 0:te], ALU.mult, reads=[bH[0], bGY], writes=[bOC])
        P.dma(k.ocT[c][:, 0:te], OC[:, 0:te], reads=[bOC])
    P.release(mk)


def super_tiles(need_ctx):
    if need_ctx:
        return ST768
    return [ST768[0], ST768[1], [(1536, 512)]]


def phase_merge(k, l, need_ctx):
    P = k.P
    mk = P.mark()
    on = [P.tile([KC, 768], BF16) for _ in range(3)]
    b_on = [P.buf(f"on{i}") for i in range(3)]
    mg = P.tile([KC, 768], BF16)
    b_mg = P.buf("mg")
    gpool = [(P.tile([3, 768], F32), P.buf(f"g{i}")) for i in range(2)]
    tpool = [(P.tile([512], F32), P.buf(f"mt{i}")) for i in range(2)]
    hpool = [(P.tile([768], F32), P.buf(f"mh{i}")) for i in range(2)]
    srcs = (k.oaT, k.obT, k.ocT)
    it = 0
    for st in super_tiles(need_ctx):
        s0 = st[0][0]
        sn = sum(b for _, b in st)
        for n in range(3):
            P.dma(on[n][:, :, 0:sn], srcs[n][:, :, s0:s0 + sn].rearrange("c p t -> p c t"), writes=[b_on[n]])
        for blk in range(4):
            wn = [load_w(k, k.w_branch[l, n], KC, blk * 512, 512) for n in range(3)]
            for jj in range(4):
                j = blk * 4 + jj
                g, gb = gpool[it % 2]
                it += 1
                P.dma(g[:, :, 0:sn], k.gatesT.rearrange("(n c) p t -> c p n t", n=3)[j][:, :, s0:s0 + sn],
                      writes=[gb])
                for (t0, tn) in st:
                    o0 = t0 - s0
                    pss = []
                    for n in range(3):
                        ps, pb = P.ps()
                        w, wb_ = wn[n]
                        for kc in range(KC):
                            P.mm(ps[:, 0:tn], w[:, kc, jj * 128:(jj + 1) * 128], on[n][:, kc, o0:o0 + tn],
                                 kc == 0, kc == KC - 1, reads=[wb_, b_on[n]], writes=[pb])
                        pss.append((ps, pb))
                    t1, tb1 = tpool[0]
                    t2, tb2 = tpool[1]
                    P.tt(t1[:, 0:tn], pss[0][0][:, 0:tn], g[:, 0, o0:o0 + tn], ALU.mult,
                         reads=[pss[0][1], gb], writes=[tb1])
                    P.tt(t2[:, 0:tn], pss[1][0][:, 0:tn], g[:, 1, o0:o0 + tn], ALU.mult,
                         reads=[pss[1][1], gb], writes=[tb2])
                    P.tt(t1[:, 0:tn], t1[:, 0:tn], t2[:, 0:tn], ALU.add, reads=[tb1, tb2], writes=[tb1])
                    P.tt(t2[:, 0:tn], pss[2][0][:, 0:tn], g[:, 2, o0:o0 + tn], ALU.mult,
                         reads=[pss[2][1], gb], writes=[tb2])
                    P.tt(mg[:, j, o0:o0 + tn], t1[:, 0:tn], t2[:, 0:tn], ALU.add,
                         reads=[tb1, tb2], writes=[b_mg])
        for blk in range(4):
            w, wb_ = load_w(k, k.w_out[l], KC, blk * 512, 512)
            for jj in range(4):
                j = blk * 4 + jj
                hh, hb = hpool[j % 2]
                P.dma(hh[:, 0:sn], k.hT[j][:, s0:s0 + sn], reads=[k.b_hT], writes=[hb])
                for (t0, tn) in st:
                    o0 = t0 - s0
                    s = 1 if is_ctx(t0) else 0
                    ps, pb = P.ps()
                    for kc in range(KC):
                        P.mm(ps[:, 0:tn], w[:, kc, jj * 128:(jj + 1) * 128], mg[:, kc, o0:o0 + tn],
                             kc == 0, kc == KC - 1, reads=[wb_, b_mg], writes=[pb])
                    P.stt(hh[:, o0:o0 + tn], ps[:, 0:tn], k.mcol[:, 2 * 16 + j, s:s + 1], hh[:, o0:o0 + tn],
                          ALU.mult, ALU.add, reads=[pb, k.b_mcol, hb], writes=[hb])
                P.dma(k.hT[j][:, s0:s0 + sn], hh[:, 0:sn], reads=[hb], writes=[k.b_hT])
    P.release(mk)


def phase_ffn(k, l, need_ctx):
    P = k.P
    mk = P.mark()
    u2 = P.tile([KC, 768], BF16)
    b_u2 = [[P.buf(f"u2_{c}_{i}") for i in range(2)] for c in range(KC)]
    hid = P.tile([44, 768], BF16)
    b_hid = P.buf("hid")
    spool = [(P.tile([512], F32), P.buf(f"fs{i}")) for i in range(2)]
    hpool = [(P.tile([768], F32), P.buf(f"fh{i}")) for i in range(2)]
    it = 0
    for st in super_tiles(need_ctx):
        s0 = st[0][0]
        sn = sum(b for _, b in st)
        ntiles = []
        for (t0, tn) in st:
            for a in range(t0, t0 + tn, 128):
                ntiles.append((a, 128))
        norm_core(k, l, 1, ntiles, TN=128,
                  dst=lambda c, t0, tn: u2[:, c, t0 - s0:t0 - s0 + tn],
                  dst_buf=lambda c, t0: b_u2[c][0 if (t0 - s0) < 512 else 1])
        for blk in range(11):
            wg, wgb = load_w(k, k.w_ffn_in[l], KC, blk * 512, 512)
            wu, wub = load_w(k, k.w_ffn_in[l], KC, FFN_H + blk * 512, 512)
            for jj in range(4):
                jf = blk * 4 + jj
                for si, (t0, tn) in enumerate(st):
                    o0 = t0 - s0
                    pg, pgb = P.ps()
                    for kc in range(KC):
                        P.mm(pg[:, 0:tn], wg[:, kc, jj * 128:(jj + 1) * 128], u2[:, kc, o0:o0 + tn],
                             kc == 0, kc == KC - 1, reads=[wgb, b_u2[kc][si]], writes=[pgb])
                    pu, pub = P.ps()
                    for kc in range(KC):
                        P.mm(pu[:, 0:tn], wu[:, kc, jj * 128:(jj + 1) * 128], u2[:, kc, o0:o0 + tn],
                             kc == 0, kc == KC - 1, reads=[wub, b_u2[kc][si]], writes=[pub])
                    sg, sgb = spool[it % 2]
                    it += 1
                    P.act(sg[:, 0:tn], pg[:, 0:tn], AF.Silu, reads=[pgb], writes=[sgb])
                    P.tt(hid[:, jf, o0:o0 + tn], sg[:, 0:tn], pu[:, 0:tn], ALU.mult,
                         reads=[sgb, pub], writes=[b_hid])
        for j in range(KC):
            w, wb_ = wbuf(k)
            wv = w[:, 0:44 * 128].rearrange("p (a n) -> p a n", n=128)
            P.dma(wv, k.w_ffn_out[l][:, j * 128:(j + 1) * 128].rearrange("(a p) n -> p a n", p=128),
                  writes=[wb_], q="pool")
            hh, hb = hpool[j % 2]
            P.dma(hh[:, 0:sn], k.hT[j][:, s0:s0 + sn], reads=[k.b_hT], writes=[hb])
            for (t0, tn) in st:
                o0 = t0 - s0
                s = 1 if is_ctx(t0) else 0
                ps, pb = P.ps()
                for kc in range(44):
                    P.mm(ps[:, 0:tn], wv[:, kc, :], hid[:, kc, o0:o0 + tn], kc == 0, kc == 43,
                         reads=[wb_, b_hid], writes=[pb])
                P.stt(hh[:, o0:o0 + tn], ps[:, 0:tn], k.mcol[:, 5 * 16 + j, s:s + 1], hh[:, o0:o0 + tn],
                      ALU.mult, ALU.add, reads=[pb, k.b_mcol, hb], writes=[hb])
            P.dma(k.hT[j][:, s0:s0 + sn], hh[:, 0:sn], reads=[hb], writes=[k.b_hT])
    P.release(mk)


def norm_core(k, l, n, tiles, dst, dst_buf, final=False, store=None, TN=256):
    P = k.P
    mk = P.mark()
    hpool = [(P.tile([KC, TN], F32), P.buf(f"hn{i}")) for i in range(2)]
    sq, sqb = P.tile([KC, TN], F32), P.buf("sq")
    rpool = [(P.tile([TN], F32), P.buf(f"rs{i}")) for i in range(2)]
    tmp_pool = [(P.tile([TN], F32), P.buf(f"tmp{i}")) for i in range(3)]
    shift_which = 0 if n == 0 else 3

    def load(i):
        t0, tn = tiles[i]
        ht, hb = hpool[i % 2]
        P.dma(ht, k.hT[:, :, t0:t0 + tn].rearrange("c p t -> p c t"), reads=[k.b_hT], writes=[hb])

    load(0)
    for i, (t0, tn) in enumerate(tiles):
        if i + 1 < len(tiles):
            load(i + 1)
        s = 1 if is_ctx(t0) else 0
        ht, hb = hpool[i % 2]
        P.act(sq, ht, AF.Square, reads=[hb], writes=[sqb])
        ps, pb = P.ps()
        for c in range(KC):
            P.mm(ps[:, 0:tn], k.ones_f, sq[:, c, :], c == 0, c == KC - 1,
                 reads=[sqb, k.b_consts], writes=[pb])
        rs, rb = rpool[i % 2]
        P.ts(rs, ps[:, 0:tn], 1.0 / D, EPS, ALU.mult, ALU.add, reads=[pb], writes=[rb])
        P.act(rs, rs, AF.Sqrt, reads=[rb], writes=[rb])
        P.op("dve", lambda e, rs=rs: e.reciprocal(rs, rs), [rb], [rb])
        for c in range(KC):
            if final:
                P.stt(dst(c, t0, tn), ht[:, c, :], k.vecs[:, V_FINALG + c:V_FINALG + c + 1], rs,
                      ALU.mult, ALU.mult, reads=[hb, rb, k.b_vecs], writes=[dst_buf(c, t0)])
            else:
                tm, tb = tmp_pool[c % 3]
                P.stt(tm, ht[:, c, :], k.modA[:, n, s, c:c + 1], rs, ALU.mult, ALU.mult,
                      reads=[hb, rb, k.b_modA], writes=[tb])
                P.act(dst(c, t0, tn), tm, AF.Identity, bias=k.mcol[:, shift_which * 16 + c, s:s + 1],
                      reads=[tb, k.b_mcol], writes=[dst_buf(c, t0)])
        if store is not None:
            store(t0, tn)
    P.release(mk)


def phase_final(k):
    P = k.P
    mk = P.mark()
    opool = [(P.tile([KC, 256], F32), P.buf(f"fo{i}")) for i in range(2)]
    cnt = {"i": -1}
    tiles = [(i * 256, 256) for i in range(TL // 256)]

    def dst(c, t0, tn):
        return opool[(t0 // 256) % 2][0][:, c, :]

    def dst_buf(c, t0):
        return opool[(t0 // 256) % 2][1]

    def store(t0, tn):
        o, ob = opool[(t0 // 256) % 2]
        P.dma(k.outT[:, :, t0:t0 + tn].rearrange("c p t -> p c t"), o, reads=[ob])
    norm_core(k, 0, 0, tiles, dst, dst_buf, final=True, store=store)
    P.release(mk)
NCH = T // 64


def phase_gla(k, l, need_ctx):
    P = k.P
    vb = l * V_PER_LAYER
    mk0 = P.mark()
    rm = P.tile([T], F32)
    b_rm = P.buf("rm")
    P.memset(rm, 1.0, writes=[b_rm])
    P.memset(rm.rearrange("p (c s) -> p c s", s=64)[:, :, 0:1], 0.0, writes=[b_rm])
    wd = P.tile([2, 1024], BF16)
    b_wd = P.buf("wd")
    P.dma(wd[0:32], k.wdec[l].rearrange("d r n -> r d n"), writes=[b_wd], q="pool")
    dec = P.tile([T], BF16)
    b_dec = P.buf("dec")
    P.dma(dec[0:32], k.decT, writes=[b_dec])
    negb = P.tile([16], F32)
    b_negb = P.buf("negb")
    P.ts(negb, k.vecs[:, vb + V_BDEC:vb + V_BDEC + 16], -1.0, None, ALU.mult, reads=[k.b_vecs], writes=[b_negb])
    qd = [[P.tile([T], BF16) for _ in range(2)] for _ in range(2)]
    kt = [[P.tile([T], BF16) for _ in range(2)] for _ in range(2)]
    b_qk = [P.buf("qdkt0"), P.buf("qdkt1")]
    kdT = [P.tile([NCH, 256], BF16) for _ in range(2)]
    b_kdT = [P.buf("kdT0"), P.buf("kdT1")]
    dcol = P.tile([2, 2, NCH], F32)
    b_dcol = [P.buf("dcol0"), P.buf("dcol1")]
    S = [P.tile([2, 512], F32) for _ in range(2)]
    bS = [P.buf("S0"), P.buf("S1")]
    Sb = [P.tile([2, 512], BF16) for _ in range(2)]
    bSb = [P.buf("Sb0"), P.buf("Sb1")]
    attT = [P.tile([64], BF16) for _ in range(2)]
    bAT = [P.buf("attT0"), P.buf("attT1")]
    mk1 = P.mark()
    og = (k.ogf, k.ogb)
    for h in range(4):
        LA = P.tile([T], F32); bLA = P.buf("LA")
        BP = P.tile([T], F32); bBP = P.buf("BP")
        TM = P.tile([T], F32); bTM = P.buf("TM")
        QL = P.tile([T], BF16); KL = P.tile([T], BF16); bQL = P.buf("QL")
        KD = P.tile([T], BF16); bKD = P.buf("KD")
        for sub in range(2):
            cc = h * 2 + sub
            P.dma(QL, k.qgT[cc], writes=[bQL])
            P.dma(KL, k.kgT[cc], writes=[bQL])
            for dr in range(2):
                last_col = 63 if dr == 0 else 0
                for (t0, tn) in TT512:
                    ps, pb = P.ps()
                    P.mm(ps[:, 0:tn], wd[0:32, dr, cc * 128:(cc + 1) * 128], dec[0:32, t0:t0 + tn], True, True,
                         reads=[b_wd, b_dec], writes=[pb])
                    P.act(LA[:, t0:t0 + tn], ps[:, 0:tn], AF.Exp, scale=-1.0,
                          bias=negb[:, dr * 8 + cc:dr * 8 + cc + 1], reads=[pb, b_negb], writes=[bLA])
                P.act(LA, LA, AF.Ln, bias=1.0, reads=[bLA], writes=[bLA])
                P.op("dve", lambda e, BP=BP, LA=LA: e.tensor_tensor_scan(
                    BP, rm, LA, 0.0, ALU.mult, ALU.add), [b_rm, bLA], [bBP])
                BP3 = BP.rearrange("p (c s) -> p c s", s=64)
                if dr == 1:
                    TM3 = TM.rearrange("p (c s) -> p c s", s=64)
                    P.tt(TM3, BP3, BP3[:, :, 63:64].to_broadcast([128, NCH, 64]), ALU.subtract,
                         reads=[bBP], writes=[bTM])
                    P.tt(BP, LA, TM, ALU.subtract, reads=[bLA, bTM], writes=[bBP])
                P.act(LA, BP, AF.Exp, scale=-1.0 / 16.0, reads=[bBP], writes=[bLA])
                P.tt(qd[dr][sub], QL, LA, ALU.mult, reads=[bQL, bLA], writes=[b_qk[dr]])
                P.copy(dcol[:, dr, sub, :], LA.rearrange("p (c s) -> p c s", s=64)[:, :, last_col],
                       reads=[bLA], writes=[b_dcol[dr]], eng="dve")
                P.act(TM, BP, AF.Exp, scale=1.0 / 16.0, reads=[bBP], writes=[bTM])
                P.tt(kt[dr][sub], KL, TM, ALU.mult, reads=[bQL, bTM], writes=[b_qk[dr]])
                P.tt(TM.rearrange("p (c s) -> p c s", s=64), BP3,
                     BP3[:, :, last_col:last_col + 1].to_broadcast([128, NCH, 64]), ALU.subtract,
                     reads=[bBP], writes=[bTM])
                P.act(TM, TM, AF.Exp, scale=1.0 / 16.0, reads=[bTM], writes=[bTM])
                P.tt(KD, KL, TM, ALU.mult, reads=[bQL, bTM], writes=[bKD])
                for c0 in range(0, NCH, 8):
                    n8 = min(8, NCH - c0)
                    pt, ptb = P.ps()
                    ptv = pt.bitcast(BF16)
                    for ci in range(n8):
                        c = c0 + ci
                        P.transpose(ptv[0:64, ci * 128:(ci + 1) * 128], KD[:, c * 64:(c + 1) * 64], k.ident_bf,
                                    reads=[bKD, k.b_consts], writes=[ptb])
                    P.copy(kdT[dr][0:64, c0:c0 + n8, sub * 128:(sub + 1) * 128],
                           ptv[0:64, 0:n8 * 128].rearrange("p (c d) -> p c d", d=128),
                           reads=[ptb], writes=[b_kdT[dr]], eng="act")
        P.release(mk1)
        V = P.tile([NCH, 512], BF16); bV = P.buf("V")
        P.dma(V[0:64], k.vg[:, h * 512:(h + 1) * 512].rearrange("(c p) n -> p c n", p=64), writes=[bV])
        OST = [[(P.tile([4, 512], F32), P.buf(f"ost{d}{i}")) for i in range(2)] for d in range(2)]
        for dr in range(2):
            P.memset(S[dr], 0.0, writes=[bS[dr]])
            P.memset(Sb[dr], 0.0, writes=[bSb[dr]])
        orders = ([32, 33, 34, 35] + list(range(32)), [35, 34, 33, 32] + list(range(31, -1, -1)))
        masks = (k.consts[0:64, 128:192], k.consts[0:64, 192:256])
        for i in range(NCH):
            for dr in range(2):
                c = orders[dr][i]
                g4 = c // 4
                last_in_group = (c % 4 == 3) if dr == 0 else (c % 4 == 0)
                ost, ostb = OST[dr][g4 % 2]
                cs = slice(c * 64, (c + 1) * 64)
                out_needed = need_ctx or c < 32
                pa, pab = P.ps()
                for sub in range(2):
                    P.mm(pa[0:64, 0:64], kt[dr][sub][:, cs], qd[dr][sub][:, cs], sub == 0, sub == 1,
                         reads=[b_qk[dr]], writes=[pab])
                P.tt(attT[dr][0:64], pa[0:64, 0:64], masks[dr], ALU.mult,
                     reads=[pab, k.b_consts], writes=[bAT[dr]])
                po, pob = P.ps()
                P.mm(po[0:64, :], attT[dr][0:64], V[0:64, c, :], True, False, reads=[bAT[dr], bV], writes=[pob])
                for sub in range(2):
                    P.mm(po[0:64, :], qd[dr][sub][:, cs], Sb[dr][:, sub, :], False, sub == 1,
                         reads=[b_qk[dr], bSb[dr]], writes=[pob])
                for sub in range(2):
                    pS, pSb = P.ps()
                    P.mm(pS, kdT[dr][0:64, c, sub * 128:(sub + 1) * 128], V[0:64, c, :], True, True,
                         reads=[b_kdT[dr], bV], writes=[pSb])
                    P.stt(S[dr][:, sub, :], S[dr][:, sub, :], dcol[:, dr, sub, c:c + 1], pS, ALU.mult, ALU.add,
                          reads=[bS[dr], b_dcol[dr], pSb], writes=[bS[dr]])
                    P.copy(Sb[dr][:, sub, :], S[dr][:, sub, :], reads=[bS[dr]], writes=[bSb[dr]], eng="dve")
                if not out_needed:
                    continue
                P.copy(ost[0:64, c % 4, :], po[0:64, :], reads=[pob], writes=[ostb], eng="act")
                if last_in_group:
                    P.dma(og[dr][g4 * 256:(g4 + 1) * 256, h * 512:(h + 1) * 512]
                          .rearrange("(c p) n -> p c n", p=64), ost[0:64], reads=[ostb])
        P.release(mk1)
    P.release(mk0)
    gbc = P.tile([512], F32)
    b_gbc = P.buf("gbc")
    P.dma(gbc, k.gbc_d[l][:, 0:512], writes=[b_gbc])
    ofp = [(P.tile([2048], F32), P.buf(f"of{i}")) for i in range(2)]
    obp = [(P.tile([2048], F32), P.buf(f"ob{i}")) for i in range(2)]
    rgp = [(P.tile([2048], F32), P.buf(f"rg{i}")) for i in range(2)]
    sq = P.tile([512], F32); b_sq = P.buf("sq")
    ssp = [(P.tile([16], F32), P.buf(f"ss{i}")) for i in range(2)]
    oap = [(P.tile([2048], BF16), P.buf(f"oa{i}")) for i in range(2)]
    stg = [(P.tile([16, 512], BF16), P.buf(f"stg{i}")) for i in range(2)]
    tiles = TT128 if need_ctx else TT128[:16]

    def load(i):
        t0 = tiles[i][0]
        P.dma(ofp[i % 2][0], k.ogf[t0:t0 + 128, :], writes=[ofp[i % 2][1]])
        P.dma(obp[i % 2][0], k.ogb[t0:t0 + 128, :], writes=[obp[i % 2][1]])
        P.dma(rgp[i % 2][0], k.rg[t0:t0 + 128, :], writes=[rgp[i % 2][1]])

    load(0)
    for i, (t0, tn) in enumerate(tiles):
        if i + 1 < len(tiles):
            load(i + 1)
        of, ofb = ofp[i % 2]
        ob, obb = obp[i % 2]
        rg, rgb = rgp[i % 2]
        ss, ssb = ssp[i % 2]
        oa, oab = oap[i % 2]
        st, stb = stg[(i // 4) % 2]
        P.tt(of, of, ob, ALU.add, reads=[ofb, obb], writes=[ofb])
        for hh in range(4):
            hs = slice(hh * 512, (hh + 1) * 512)
            P.op("dve", lambda e, of=of, ss=ss, hs=hs, hh=hh: e.scalar_tensor_tensor(
                sq, of[:, hs], 1.0, of[:, hs], ALU.mult, ALU.mult, accum_out=ss[:, hh:hh + 1]),
                [ofb], [b_sq, ssb])
        P.act(ss[:, 4:8], ss[:, 0:4], AF.Sqrt, scale=1.0 / 512.0, bias=EPS, reads=[ssb], writes=[ssb])
        P.op("dve", lambda e, ss=ss: e.reciprocal(ss[:, 8:12], ss[:, 4:8]), [ssb], [ssb])
        P.act(ss[:, 12:16], ss[:, 8:12], AF.Copy, reads=[ssb], writes=[ssb])
        for hh in range(4):
            hs = slice(hh * 512, (hh + 1) * 512)
            P.stt(of[:, hs], of[:, hs], ss[:, 12 + hh:13 + hh], gbc, ALU.mult, ALU.mult,
                  reads=[ofb, ssb, b_gbc], writes=[ofb])
        P.tt(oa, of, rg, ALU.mult, reads=[ofb, rgb], writes=[oab])
        for q4 in range(4):
            pt, ptb = P.ps()
            ptv = pt.bitcast(BF16)
            for j in range(4):
                f = q4 * 4 + j
                P.transpose(ptv[:, j * 128:(j + 1) * 128], oa[:, f * 128:(f + 1) * 128], k.ident_bf,
                            reads=[oab, k.b_consts], writes=[ptb])
            o0 = (i % 4) * 128
            P.copy(st[:, q4 * 4:(q4 + 1) * 4, o0:o0 + 128], ptv[:, 0:512].rearrange("p (f t) -> p f t", t=128),
                   reads=[ptb], writes=[stb], eng="act")
        if i % 4 == 3 or i == len(tiles) - 1:
            g0 = (i // 4) * 512
            gn = t0 + 128 - g0
            P.dma(k.oaT[:, :, g0:g0 + gn].rearrange("c p t -> p c t"), st[:, :, 0:gn], reads=[stb])
    P.release(mk0)
def _col(v):
    v = np.asarray(v, np.float32)
    return np.ascontiguousarray(v.reshape(-1, 128).T)


def _rope_tables():
    rows = TL // 64
    row = np.repeat(np.arange(rows, dtype=np.float32), 64)
    col = np.tile(np.arange(64, dtype=np.float32), rows)
    inv = (np.float32(10000.0) ** (-np.arange(32, dtype=np.float32) / np.float32(32))).astype(np.float32)
    ang = np.concatenate([row[:, None] * inv, col[:, None] * inv], axis=-1).astype(np.float32)
    cs = np.concatenate([np.cos(ang), np.sin(ang)], axis=-1).astype(np.float32)
    return np.ascontiguousarray(cs.reshape(16, 128, 128).transpose(1, 0, 2))


def _consts():
    c = np.zeros((128, 320), np.float32)
    c[:, 0:128] = np.eye(128, dtype=np.float32)
    j = np.arange(64)[:, None]
    i = np.arange(64)[None, :]
    c[0:64, 128:192] = (j <= i)
    c[0:64, 192:256] = (j >= i)
    c[:, 256:320] = 1.0
    c[:, 256] = 0.0
    return c


def pack_inputs(inputs, b, nl=DEPTH):
    f = lambda a: np.asarray(a, np.float32)
    vecs = np.zeros((128, NV), np.float32)
    for l in range(DEPTH):
        vb = l * V_PER_LAYER
        vecs[:, vb + V_BMOD:vb + V_BMOD + 96] = _col(f(inputs["b_mod"])[l])
        vecs[:, vb + V_GMIX:vb + V_GMIX + 16] = _col(f(inputs["norm_mix_g"])[l])
        vecs[:, vb + V_GFFN:vb + V_GFFN + 16] = _col(f(inputs["norm_ffn_g"])[l])
        vecs[:, vb + V_BDEC:vb + V_BDEC + 16] = _col(f(inputs["gla_b_decay"])[l].reshape(-1))
        vecs[:, vb + V_CONVW:vb + V_CONVW + 64] = _col(f(inputs["conv_w"])[l].reshape(-1))
        vecs[:, vb + V_CONVB:vb + V_CONVB + 16] = _col(f(inputs["conv_b"])[l])
        vecs[:, vb + V_LBA:vb + V_LBA + 32] = _col(f(inputs["lru_b_a"])[l].reshape(-1))
        vecs[:, vb + V_LBI:vb + V_LBI + 32] = _col(f(inputs["lru_b_i"])[l].reshape(-1))
        vecs[:, vb + V_LAM:vb + V_LAM + 32] = _col(f(inputs["lru_lambda"])[l].reshape(-1))
        vecs[:, vb + V_BMRG:vb + V_BMRG + 48] = _col(f(inputs["b_merge"])[l].reshape(-1))
    vecs[:, V_FINALG:V_FINALG + 16] = _col(f(inputs["final_norm_g"]))
    vecs[:, V_C:V_C + 16] = _col(f(inputs["c"])[b])
    vecs[:, V_CCTX:V_CCTX + 16] = _col(f(inputs["c_ctx"]))
    gbc = np.zeros((DEPTH, 128, 768), np.float32)
    gbc[:, :, 0:512] = f(inputs["gla_norm_g"])[:, None, :]
    gbc[:, :, 512:640] = f(inputs["q_norm_g"])[:, None, :]
    gbc[:, :, 640:768] = f(inputs["k_norm_g"])[:, None, :]
    h0 = np.concatenate([f(inputs["x"])[b], f(inputs["ctx"])[b]], axis=0)
    hT0 = np.ascontiguousarray(h0.T).reshape(KC, 128, T)
    wd = f(inputs["gla_w_decay"])
    wdec = np.zeros((DEPTH, 2, 32, 1024), np.float32)
    wdec[:, 0, 0:16] = wd[:, 0]
    wdec[:, 1, 16:32] = wd[:, 1]
    lru_w = np.ascontiguousarray(np.stack([f(inputs["lru_w_a"]), f(inputs["lru_w_i"])], axis=1))
    return {
        "hT0": hT0, "vecs": vecs, "gbc": gbc, "rope": _rope_tables(), "consts": _consts(),
        "w_mod": f(inputs["w_mod"])[:nl], "w_in": f(inputs["w_in"])[:nl], "wdec": wdec[:nl], "lru_w": lru_w[:nl],
        "w_branch": f(inputs["w_branch"])[:nl], "w_out": f(inputs["w_out"])[:nl],
        "w_ffn_in": f(inputs["w_ffn_in"])[:nl], "w_ffn_out": f(inputs["w_ffn_out"])[:nl],
    }


_CACHE = {}


def kernel(**inputs):
    if "nc" not in _CACHE:
        _CACHE["nc"] = build_program()[0]
    nc = _CACHE["nc"]
    in_maps = [pack_inputs(inputs, b) for b in range(4)]
    res = run_bass_kernel_spmd(nc, in_maps, core_ids=list(range(4)))
    out = np.stack([np.ascontiguousarray(res.results[b]["outT"].reshape(D, TL).T) for b in range(4)], axis=0)
    return out.astype(np.float32)
```
